# Optimizing a Trainium2 kernel written in Bass

```python
import jax, jax.numpy as jnp
from jax import lax
import numpy as np

D_MODEL = 1024
BATCH = 16
SEQ = 4096
DEPTH = 4
DEC_BATCH = 2
DEC_SEQ = 16384
PAST_LEN = 128

D_MIX = D_MODEL
A_WIDTH = D_MIX // 2
A_GROUPS = 8
CONV_W = 3
B_WIDTH = D_MIX - A_WIDTH
B_HEADS = 4
B_HEAD_DIM = B_WIDTH // B_HEADS
CHUNK = 128
PROJ_OUT = 4 * A_WIDTH + 3 * B_WIDTH
ALPHA = (2.0 * DEPTH) ** 0.25
BETA = (8.0 * DEPTH) ** -0.25
LN_EPS = 1e-5

kernel_name = "hybrid_shortconv_gmlp_encoder"


def _layernorm(x, g, b):
    xf = x.astype(jnp.float32)
    mu = jnp.mean(xf, axis=-1, keepdims=True)
    var = jnp.mean(jnp.square(xf - mu), axis=-1, keepdims=True)
    y = (xf - mu) * lax.rsqrt(var + LN_EPS)
    return (y * g.astype(jnp.float32) + b.astype(jnp.float32)).astype(x.dtype)


def _layer(x, w_in, conv_w, conv_b, sg_w, sg_b, sg_norm_g, sg_norm_b, w_out, ln_g, ln_b):
    bsz, seq, _ = x.shape
    h = jnp.einsum('bsd,de->bse', x, w_in)
    a_in, a_b, a_c, a_z, b_u, b_v, b_z = jnp.split(
        h, [A_WIDTH, 2 * A_WIDTH, 3 * A_WIDTH, 4 * A_WIDTH,
            4 * A_WIDTH + B_WIDTH, 4 * A_WIDTH + 2 * B_WIDTH], axis=-1)

    t = a_c * a_in
    tp = jnp.pad(t, ((0, 0), (1, 1), (0, 0)))
    conv = (conv_w[0] * tp[:, :-2] + conv_w[1] * tp[:, 1:-1]
            + conv_w[2] * tp[:, 2:] + conv_b)
    out_a = a_b * conv * jax.nn.silu(a_z)

    n_chunks = seq // CHUNK
    u = jax.nn.gelu(b_u)
    v = jax.nn.gelu(b_v).reshape(bsz, n_chunks, CHUNK, B_HEADS, B_HEAD_DIM)
    v = _layernorm(v, sg_norm_g.reshape(B_HEADS, B_HEAD_DIM), sg_norm_b.reshape(B_HEADS, B_HEAD_DIM))
    s = jnp.einsum('hpq,bnqhd->bnphd', sg_w, v) + jnp.transpose(sg_b)[None, None, :, :, None]
    s = s.reshape(bsz, seq, B_WIDTH)
    out_b = u * s * jax.nn.silu(b_z)

    y = jnp.einsum('bse,ed->bsd', jnp.concatenate([out_a, out_b], axis=-1), w_out)
    return _layernorm(ALPHA * x + y, ln_g, ln_b)


def _trunk(x, w_in, conv_w, conv_b, sg_w, sg_b, sg_norm_g, sg_norm_b, w_out, ln_g, ln_b):
    for l in range(DEPTH):
        x = _layer(x, w_in[l], conv_w[l], conv_b[l], sg_w[l], sg_b[l],
                   sg_norm_g[l], sg_norm_b[l], w_out[l], ln_g[l], ln_b[l])
    return x


def setup_inputs(seed: int = 0) -> dict:
    key = jax.random.key(seed)
    ks = jax.random.split(key, 12)
    f32 = jnp.float32
    x_prompt = jax.random.normal(ks[0], (BATCH, SEQ, D_MODEL), f32)
    x_sample = jax.random.normal(ks[1], (DEC_BATCH, DEC_SEQ, D_MODEL), f32)
    w_in = jax.random.normal(ks[2], (DEPTH, D_MODEL, PROJ_OUT), f32) * D_MODEL ** -0.5
    conv_w = jax.random.normal(ks[3], (DEPTH, CONV_W, A_WIDTH), f32) * CONV_W ** -0.5
    conv_b = jax.random.normal(ks[4], (DEPTH, A_WIDTH), f32) * 0.02
    sg_w = jax.random.normal(ks[5], (DEPTH, B_HEADS, CHUNK, CHUNK), f32) * CHUNK ** -0.5
    sg_b = 1.0 + 0.1 * jax.random.normal(ks[6], (DEPTH, B_HEADS, CHUNK), f32)
    sg_norm_g = 1.0 + 0.02 * jax.random.normal(ks[7], (DEPTH, B_WIDTH), f32)
    sg_norm_b = 0.02 * jax.random.normal(ks[8], (DEPTH, B_WIDTH), f32)
    w_out = jax.random.normal(ks[9], (DEPTH, D_MIX, D_MODEL), f32) * (D_MIX ** -0.5) * BETA
    ln_g = 1.0 + 0.02 * jax.random.normal(ks[10], (DEPTH, D_MODEL), f32)
    ln_b = 0.02 * jax.random.normal(ks[11], (DEPTH, D_MODEL), f32)
    return {"x_prompt": x_prompt, "x_sample": x_sample, "w_in": w_in, "conv_w": conv_w,
            "conv_b": conv_b, "sg_w": sg_w, "sg_b": sg_b, "sg_norm_g": sg_norm_g,
            "sg_norm_b": sg_norm_b, "w_out": w_out, "ln_g": ln_g, "ln_b": ln_b}


def reference(x_prompt, x_sample, w_in, conv_w, conv_b, sg_w, sg_b, sg_norm_g, sg_norm_b,
              w_out, ln_g, ln_b):
    y_prompt = _trunk(x_prompt, w_in, conv_w, conv_b, sg_w, sg_b, sg_norm_g, sg_norm_b,
                      w_out, ln_g, ln_b)
    y_sample = _trunk(x_sample, w_in, conv_w, conv_b, sg_w, sg_b, sg_norm_g, sg_norm_b,
                      w_out, ln_g, ln_b)
    return (y_prompt, y_sample)
```

```python
from contextlib import ExitStack

import numpy as np
import concourse.bass as bass
import concourse.mybir as mybir
from concourse.bass_utils import run_bass_kernel_spmd

F32 = mybir.dt.float32
BF16 = mybir.dt.bfloat16
AF = mybir.ActivationFunctionType
ALU = mybir.AluOpType

D = 1024
KC = 8
NCOLS = 3584
DEPTH = 4
ALPHA = (2.0 * DEPTH) ** 0.25
LN_EPS = 1e-5
CG_AIN, CG_AB, CG_AC, CG_AZ, CG_BU, CG_BV, CG_BZ = range(7)

ENGS = ("pe", "act", "dve", "pool", "sp")


class Buf:
    __slots__ = ("name", "w", "r")

    def __init__(self, name):
        self.name = name
        self.w = None
        self.r = {}


class Prog:
    def __init__(self, n_dma_sp=24, n_dma_pool=16):
        self.streams = {e: [] for e in ENGS}
        self.cnt = {e: 0 for e in ENGS}
        self.waited = {e: {} for e in ENGS}
        self.dma_pool = {
            "sp": ["dsp%d" % i for i in range(n_dma_sp)],
            "pool": ["dpl%d" % i for i in range(n_dma_pool)],
        }
        self.dma_next = {"sp": 0, "pool": 0}
        self.dma_cnt = {}
        self.n_ops = {e: 0 for e in ENGS}
        self.debug = False
        self.tags = {}

    def sem_keys(self):
        return list(ENGS) + self.dma_pool["sp"] + self.dma_pool["pool"]

    def _need(self, eng, deps):
        w = self.waited[eng]
        for s, v in deps:
            if v <= w.get(s, 0):
                continue
            w[s] = v
            self.streams[eng].append(("wait", s, v))

    def _deps(self, eng, reads, writes, is_dma):
        deps = []
        for b in reads:
            if b.w is not None and not (eng == "pe" and b.w[0] == "pe"):
                deps.append(b.w)
        for b in writes:
            if b.w is not None and not (eng == "pe" and b.w[0] == "pe"):
                deps.append(b.w)
            for s, v in b.r.items():
                if not (eng == "pe" and s == "pe"):
                    deps.append((s, v))
        return deps

    def op(self, eng, fns, reads=(), writes=()):
        if callable(fns):
            fns = [fns]
        self._need(eng, self._deps(eng, reads, writes, False))
        self.cnt[eng] += 1
        v = self.cnt[eng]
        if self.debug:
            import sys as _sys
            f = _sys._getframe(1)
            self.tags[(eng, v)] = "%s:%d" % (f.f_code.co_name, f.f_lineno)
        self.streams[eng].append(("op", fns, eng, 1))
        self.n_ops[eng] += len(fns)
        for b in reads:
            b.r[eng] = v
        for b in writes:
            b.w = (eng, v)
            b.r = {}
        return (eng, v)

    def dma(self, q, fn, reads=(), writes=()):
        deps = self._deps(q, reads, writes, True)
        pool = self.dma_pool[q]
        s = pool[self.dma_next[q] % len(pool)]
        self.dma_next[q] += 1
        k = self.dma_cnt.get(s, 0)
        if k > 0:
            deps.append((s, 16 * k))
        self._need(q, deps)
        k += 1
        self.dma_cnt[s] = k
        self.streams[q].append(("op", [fn], s, 16))
        self.n_ops[q] += 1
        for b in reads:
            b.r[s] = 16 * k
        for b in writes:
            b.w = (s, 16 * k)
            b.r = {}
        return (s, 16 * k)

    def wait_all(self, eng, bufs):
        deps = []
        for b in bufs:
            if b.w is not None:
                deps.append(b.w)
        self._need(eng, deps)

    def replay(self, eng, e, sems):
        for item in self.streams[eng]:
            if item[0] == "wait":
                e.wait_ge(sems[item[1]], item[2])
            else:
                _, fns, s, inc = item
                ins = None
                for fn in fns:
                    ins = fn(e)
                ins.then_inc(sems[s], inc)


def sb_ap(t, offset, dims):
    return bass.AP(t, offset, [list(d) for d in dims])


def build_program(n_layers, segs, n_out_tokens, store_map, debug=False):
    nc = bass.Bass("TRN2", target_bir_lowering=False)
    n_tiles = sum(s["n_chunks"] for s in segs)
    n_tok = n_tiles * 128
    assert all(s["n_chunks"] % 4 == 0 for s in segs)

    xin = nc.dram_tensor("xin", [n_tok, D], F32, kind="ExternalInput").ap()
    w_in_d = nc.dram_tensor("w_in", [n_layers, D, NCOLS], F32, kind="ExternalInput").ap()
    w_out_d = nc.dram_tensor("w_out", [n_layers, D, D], F32, kind="ExternalInput").ap()
    convw_d = nc.dram_tensor("convw", [n_layers, 128, 12], F32, kind="ExternalInput").ap()
    convb_d = nc.dram_tensor("convb", [n_layers, 128, 4], F32, kind="ExternalInput").ap()
    sgwT_d = nc.dram_tensor("sgwT", [n_layers, 128, 512], F32, kind="ExternalInput").ap()
    sgb_d = nc.dram_tensor("sgb", [n_layers, 512], F32, kind="ExternalInput").ap()
    sgg_d = nc.dram_tensor("sgg", [n_layers, 128, 4], F32, kind="ExternalInput").ap()
    sgnb_d = nc.dram_tensor("sgnb", [n_layers, 128, 4], F32, kind="ExternalInput").ap()
    lng_d = nc.dram_tensor("lng", [n_layers, D], F32, kind="ExternalInput").ap()
    lnb_d = nc.dram_tensor("lnb", [n_layers, D], F32, kind="ExternalInput").ap()
    ident_d = nc.dram_tensor("ident", [128, 128], F32, kind="ExternalInput").ap()
    mask_d = nc.dram_tensor("mask", [128, 2], F32, kind="ExternalInput").ap()
    yout = nc.dram_tensor("yout", [n_out_tokens, D], F32, kind="ExternalOutput").ap()
    xs = [nc.dram_tensor("xs%d" % i, [n_tok, D], F32).ap() for i in range(2)]

    P = Prog()
    P.debug = debug

    with ExitStack() as es:
        def sb(name, shape, dt):
            return es.enter_context(nc.sbuf_tensor(name, shape, dt))

        win = sb("win", [128, KC, NCOLS], BF16)
        wout = sb("wout", [128, KC, D], BF16)
        xT = sb("xT", [128, KC, 512], BF16)
        NXB = 4
        xb = [sb("xb%d" % i, [128, D], BF16) for i in range(NXB)]
        NXR = 3
        xres = [sb("xres%d" % i, [128, D], F32) for i in range(NXR)]
        NZ = 4
        zt = [sb("z%d" % i, [128, D], F32) for i in range(NZ)]
        Tt = [sb("T%d" % i, [128, 4, 516], F32) for i in range(2)]
        Gt = sb("G", [128, 4, 512], F32)
        outA = sb("outA", [128, 4, 512], BF16)
        outB = [sb("outB%d" % i, [128, 4, 512], BF16) for i in range(2)]
        uz = sb("uz", [128, 4, 512], F32)
        NTMP = 8
        tmp = [sb("tmp%d" % i, [128, 512], F32) for i in range(NTMP)]
        vn = [sb("vn%d" % i, [128, 512], BF16) for i in range(4)]
        lnG = sb("lnG", [128, D], F32)
        lnB = sb("lnB", [128, D], F32)
        identf = sb("identf", [128, 128], F32)
        identb = sb("identb", [128, 128], BF16)
        onesf = sb("onesf", [128, 128], F32)
        maskt = sb("maskt", [128, 2], F32)
        neghalf = sb("neghalf", [128, 4], F32)
        sgwTb = [sb("sgwTb%d" % i, [128, 512], BF16) for i in range(2)]
        Ct = [sb("C%d" % i, [128, 512], F32) for i in range(2)]
        convw = [sb("convw%d" % i, [128, 12], F32) for i in range(2)]
        convb = [sb("convb%d" % i, [128, 4], F32) for i in range(2)]
        sgg = [sb("sgg%d" % i, [128, 4], F32) for i in range(2)]
        sgnb = [sb("sgnb%d" % i, [128, 4], F32) for i in range(2)]
        NST = 4
        st_v = [sb("stv%d" % i, [128, 4, 6], F32) for i in range(NST)]
        mv_v = [sb("mvv%d" % i, [128, 4, 2], F32) for i in range(NST)]
        rs_v = [sb("rsv%d" % i, [128, 4], F32) for i in range(NST)]
        nm_v = [sb("nmv%d" % i, [128, 4], F32) for i in range(NST)]
        st_z = [sb("stz%d" % i, [128, 2, 6], F32) for i in range(NST)]
        mv_z = [sb("mvz%d" % i, [128, 2], F32) for i in range(NST)]
        rs_z = [sb("rsz%d" % i, [128, 1], F32) for i in range(NST)]
        nm_z = [sb("nmz%d" % i, [128, 1], F32) for i in range(NST)]

        NPS = 6
        ps = [es.enter_context(nc.psum_tensor("ps%d" % i, [128, 512], F32)) for i in range(NPS)]
        pt = [es.enter_context(nc.psum_tensor("pt%d" % i, [128, D], BF16)) for i in range(2)]

        b_win = [[Buf("win%d_%d" % (kc, cg)) for cg in range(7)] for kc in range(KC)]
        b_wout = [Buf("wout%d" % kc) for kc in range(KC)]
        b_xT = Buf("xT")
        b_xb = [Buf("xb%d" % i) for i in range(NXB)]
        b_xres = [Buf("xres%d" % i) for i in range(NXR)]
        b_z = [Buf("z%d" % i) for i in range(NZ)]
        b_T = [Buf("T%d" % i) for i in range(2)]
        b_G = [Buf("G%d" % i) for i in range(4)]
        b_outA = Buf("outA")
        b_outB = [Buf("outB%d" % i) for i in range(2)]
        b_uz = [Buf("uz%d" % i) for i in range(4)]
        b_tmp = [Buf("tmp%d" % i) for i in range(NTMP)]
        b_vn = [Buf("vn%d" % i) for i in range(4)]
        b_lnGB = Buf("lnGB")
        b_const = Buf("const")
        b_par = [Buf("par%d" % i) for i in range(2)]
        b_stv = [Buf("stv%d" % i) for i in range(NST)]
        b_stz = [Buf("stz%d" % i) for i in range(NST)]
        b_ps = [Buf("ps%d" % i) for i in range(NPS)]
        b_pt = [Buf("pt%d" % i) for i in range(2)]
        b_xs = [[Buf("xs%d_%d" % (i, t)) for t in range(n_tiles)] for i in range(2)]
        b_yout = [Buf("yout%d" % t) for t in range(n_tiles)]

        ctr = {"ps": 0, "pt": 0, "tmp": 0, "xb": 0, "xres": 0, "z": 0, "stv": 0, "stz": 0}

        tmp_hold = set()

        def nxt(name, n):
            while True:
                i = ctr[name] % n
                ctr[name] += 1
                if name == "tmp" and i in tmp_hold:
                    continue
                return i

        groups = []
        t0 = 0
        for si, s in enumerate(segs):
            ng = s["n_chunks"] // 4
            for g in range(ng):
                groups.append({
                    "tile0": t0 + 4 * g, "seg": si, "first": g == 0, "last": g == ng - 1,
                    "mask_l": (s.get("mask_l_tok") is not None and s["mask_l_tok"] // 512 == g),
                    "mask_l_col": (s["mask_l_tok"] % 512 + 2) if s.get("mask_l_tok") is not None else None,
                    "mask_r": (s.get("mask_r_tok") is not None and s["mask_r_tok"] // 512 == g),
                    "mask_r_col": (s["mask_r_tok"] % 512 + 2) if s.get("mask_r_tok") is not None else None,
                    "edge": [(min(4 * g + c, s["n_chunks"] - 1 - (4 * g + c))
                              if min(4 * g + c, s["n_chunks"] - 1 - (4 * g + c)) < s.get("halo", 0) else None)
                             for c in range(4)],
                })
            t0 += s["n_chunks"]
        NG = len(groups)

        def active_chunks(it):
            l, g = items[it]
            n_skip = 2 if l == n_layers - 1 else (0 if l == 0 else 1)
            return [c for c in range(4) if groups[g]["edge"][c] is None or groups[g]["edge"][c] >= n_skip]
        assert NG >= 4 or n_layers == 1
        items = [(l, g) for l in range(n_layers) for g in range(NG)]

        tile_src = {}

        def src_tile(l, t):
            if t in tile_src:
                return tile_src[t]
            return (xin, None)

        def src_of(l):
            return (xin, None) if l == 0 else (xs[(l - 1) % 2], b_xs[(l - 1) % 2])

        def load_consts():
            P.dma("sp", lambda e: e.dma_start(out=identf[:], in_=ident_d[:, :]), [], [b_const])
            P.dma("pool", lambda e: e.dma_start(out=identb[:], in_=ident_d[:, :]), [], [b_const])
            P.dma("sp", lambda e: e.dma_start(out=maskt[:], in_=mask_d[:, :]), [], [b_const])
            P.op("dve", lambda e: e.memset(onesf[:], 1.0), [], [b_const])
            P.op("dve", lambda e: e.memset(neghalf[:], -0.5), [], [b_const])

        def load_win_slab(l, kc, cg):
            P.dma("pool",
                  lambda e: e.dma_start(out=win[:, kc, cg * 512:(cg + 1) * 512],
                                        in_=w_in_d[l, kc * 128:(kc + 1) * 128, cg * 512:(cg + 1) * 512]),
                  [], [b_win[kc][cg]])

        def load_wout_slab(l, kc):
            P.dma("pool",
                  lambda e: e.dma_start(out=wout[:, kc, :], in_=w_out_d[l, kc * 128:(kc + 1) * 128, :]),
                  [], [b_wout[kc]])

        def load_ln(l):
            P.dma("sp", lambda e: e.dma_start(out=lnG[:], in_=lng_d[l, :].partition_broadcast(128)),
                  [], [b_lnGB])
            P.dma("sp", lambda e: e.dma_start(out=lnB[:], in_=lnb_d[l, :].partition_broadcast(128)),
                  [], [b_lnGB])

        def setup_layer_dma(l):
            par = l % 2
            bp = b_par[par]
            P.dma("sp", lambda e: e.dma_start(out=convw[par][:], in_=convw_d[l, :, :]), [], [bp])
            P.dma("sp", lambda e: e.dma_start(out=convb[par][:], in_=convb_d[l, :, :]), [], [bp])
            P.dma("sp", lambda e: e.dma_start(out=sgg[par][:], in_=sgg_d[l, :, :]), [], [bp])
            P.dma("sp", lambda e: e.dma_start(out=sgnb[par][:], in_=sgnb_d[l, :, :]), [], [bp])
            P.dma("pool", lambda e: e.dma_start(out=sgwTb[par][:], in_=sgwT_d[l, :, :]), [], [bp])
            i1 = nxt("tmp", NTMP)
            tmp_hold.add(i1)
            i2 = nxt("tmp", NTMP)
            tmp_hold.add(i2)
            P.dma("sp", lambda e: e.dma_start(out=tmp[i1][:], in_=sgwT_d[l, :, :]), [], [b_tmp[i1]])
            P.dma("sp", lambda e: e.dma_start(out=tmp[i2][:], in_=sgb_d[l, :].partition_broadcast(128)),
                  [], [b_tmp[i2]])
            return i1, i2

        def setup_layer_compute(l, i1, i2):
            par = l % 2
            bp = b_par[par]
            ip = nxt("ps", NPS)
            P.op("pe", lambda e: e.matmul(ps[ip][:], onesf[:], tmp[i1][:], start=True, stop=True),
                 [b_const, b_tmp[i1]], [b_ps[ip]])
            for h in range(4):
                P.op("dve",
                     lambda e, h=h: e.scalar_tensor_tensor(
                         out=Ct[par][:, h * 128:(h + 1) * 128], in0=ps[ip][:, h * 128:(h + 1) * 128],
                         scalar=sgnb[par][:, h:h + 1], in1=tmp[i2][:, h * 128:(h + 1) * 128],
                         op0=ALU.mult, op1=ALU.add),
                     [b_ps[ip], b_tmp[i2], bp], [bp])
            tmp_hold.discard(i1)
            tmp_hold.discard(i2)

        def load_xb(it):
            l, g = items[it]
            src, bsrc = src_of(l)
            slots = []

            def one(c):
                t = groups[g]["tile0"] + c
                i = nxt("xb", NXB)
                slots.append(i)
                tsrc, tb = src_tile(l, t)
                assert l == 0 or tb is not None, "load emitted before its producer store"
                P.dma("pool",
                      lambda e: e.dma_start(out=xb[i][:], in_=tsrc[t * 128:(t + 1) * 128, :]),
                      [tb] if tb is not None else [], [b_xb[i]])
            for c in range(4):
                one(c)
            return slots

        def transposes(it, slots):
            def one(c):
                i = slots[c]
                j = nxt("pt", 2)
                P.op("pe",
                     [(lambda e, kc=kc: e.transpose(pt[j][:, kc * 128:(kc + 1) * 128],
                                                    xb[i][:, kc * 128:(kc + 1) * 128], identb[:]))
                      for kc in range(KC)],
                     [b_xb[i], b_const], [b_pt[j]])
                P.op("dve",
                     lambda e: e.tensor_copy(
                         sb_ap(xT, c * 128, [[KC * 512, 128], [512, KC], [1, 128]]),
                         sb_ap(pt[j], 0, [[D, 128], [128, KC], [1, 128]])),
                     [b_pt[j]], [b_xT])
            return [lambda c=c: one(c) for c in range(4)]

        def fm_group(cg, cb, n0=0, n1=512):
            ip = nxt("ps", NPS)
            col = cg * 512 + cb * 128
            P.op("pe",
                 [(lambda e, kc=kc: e.matmul(ps[ip][:, n0:n1], win[:, kc, col:col + 128], xT[:, kc, n0:n1],
                                             start=(kc == 0), stop=(kc == KC - 1)))
                  for kc in range(KC)],
                 [b_win[kc][cg] for kc in range(KC)] + [b_xT], [b_ps[ip]])
            return ip

        def s1p1(it, tails):
            l, g = items[it]
            tb = it % 2

            tn = [c for c in range(4)
                  if groups[g]["edge"][c] is None or groups[g]["edge"][c] >= 1 or l <= n_layers - 3]
            t0c, t1c = tn[0] * 128, (tn[-1] + 1) * 128

            def one(cb):
                ia = fm_group(CG_AIN, cb, t0c, t1c)
                ic = fm_group(CG_AC, cb, t0c, t1c)
                k = nxt("tmp", NTMP)
                P.op("act", lambda e: e.activation(out=tmp[k][:], in_=ps[ia][:], func=AF.Identity),
                     [b_ps[ia]], [b_tmp[k]])
                P.op("dve",
                     lambda e: e.tensor_tensor(out=Tt[tb][:, cb, 2:514], in0=ps[ic][:],
                                               in1=tmp[k][:], op=ALU.mult),
                     [b_ps[ic], b_tmp[k]], [b_T[tb]])
            for cb in range(4):
                one(cb)
                if cb < len(tails):
                    tails[cb]()
            gi = groups[g]
            if gi["mask_l"]:
                c0 = gi["mask_l_col"]
                P.op("dve",
                     lambda e: e.tensor_scalar(out=Tt[tb][:, :, c0:c0 + 1], in0=Tt[tb][:, :, c0:c0 + 1],
                                               scalar1=maskt[:, 0:1], scalar2=None, op0=ALU.mult),
                     [b_T[tb], b_const], [b_T[tb]])
            if gi["mask_r"]:
                c1 = gi["mask_r_col"]
                P.op("dve",
                     lambda e: e.tensor_scalar(out=Tt[tb][:, :, c1:c1 + 1], in0=Tt[tb][:, :, c1:c1 + 1],
                                               scalar1=maskt[:, 1:2], scalar2=None, op0=ALU.mult),
                     [b_T[tb], b_const], [b_T[tb]])
            pb = 1 - tb
            if gi["first"]:
                P.op("dve", lambda e: e.memset(Tt[tb][:, :, 1:2], 0.0), [], [b_T[tb]])
            else:
                P.op("dve", lambda e: e.tensor_copy(Tt[tb][:, :, 1:2], Tt[pb][:, :, 513:514]),
                     [b_T[pb]], [b_T[tb]])
                P.op("dve", lambda e: e.tensor_copy(Tt[pb][:, :, 514:515], Tt[tb][:, :, 2:3]),
                     [b_T[tb]], [b_T[pb]])
            if gi["last"]:
                P.op("dve", lambda e: e.memset(Tt[tb][:, :, 514:515], 0.0), [], [b_T[tb]])

        def s1v(it):
            act = active_chunks(it)

            def one(c):
                if c not in act:
                    return (lambda: None), (lambda: None)
                ip = nxt("ps", NPS)
                P.op("pe",
                     [(lambda e, kc=kc: e.matmul(ps[ip][:], xT[:, kc, c * 128:(c + 1) * 128],
                                                 win[:, kc, CG_BV * 512:(CG_BV + 1) * 512],
                                                 start=(kc == 0), stop=(kc == KC - 1)))
                      for kc in range(KC)],
                     [b_win[kc][CG_BV] for kc in range(KC)] + [b_xT], [b_ps[ip]])
                k = nxt("tmp", NTMP)
                tmp_hold.add(k)
                P.op("act", lambda e: e.activation(out=tmp[k][:], in_=ps[ip][:], func=AF.Gelu_apprx_tanh),
                     [b_ps[ip]], [b_tmp[k]])
                s = nxt("stv", NST)

                def stats():
                    for h in range(4):
                        P.op("dve", lambda e, h=h: e.bn_stats(st_v[s][:, h, :], tmp[k][:, h * 128:(h + 1) * 128]),
                             [b_tmp[k]], [b_stv[s]])
                    for h in range(4):
                        P.op("dve", lambda e, h=h: e.bn_aggr(mv_v[s][:, h, :], st_v[s][:, h, :]),
                             [b_stv[s]], [b_stv[s]])
                    P.op("pool",
                         lambda e: e.tensor_scalar(out=rs_v[s][:], in0=mv_v[s][:, :, 1], scalar1=LN_EPS,
                                                   scalar2=None, op0=ALU.add),
                         [b_stv[s]], [b_stv[s]])
                    P.op("pool",
                         lambda e: e.tensor_tensor(out=rs_v[s][:], in0=rs_v[s][:], in1=neghalf[:, 0:4], op=ALU.pow),
                         [b_stv[s], b_const], [b_stv[s]])
                    P.op("pool",
                         lambda e: e.tensor_tensor(out=nm_v[s][:], in0=mv_v[s][:, :, 0], in1=rs_v[s][:], op=ALU.mult),
                         [b_stv[s]], [b_stv[s]])
                    P.op("pool",
                         lambda e: e.tensor_scalar(out=nm_v[s][:], in0=nm_v[s][:], scalar1=-1.0, scalar2=None,
                                                   op0=ALU.mult),
                         [b_stv[s]], [b_stv[s]])

                def norm():
                    for h in range(4):
                        P.op("act",
                             lambda e, h=h: e.activation(
                                 out=vn[c][:, h * 128:(h + 1) * 128], in_=tmp[k][:, h * 128:(h + 1) * 128],
                                 func=AF.Identity, scale=rs_v[s][:, h:h + 1], bias=nm_v[s][:, h:h + 1]),
                             [b_tmp[k], b_stv[s]], [b_vn[c]])
                    tmp_hold.discard(k)
                return stats, norm
            return [one(c) for c in range(4)]

        def conv(it, ks=None, front_only=False):
            l, g = items[it]
            tb = it % 2
            par = l % 2
            held = []

            def front(cb):
                k = nxt("tmp", NTMP)
                P.op("act",
                     lambda e: e.activation(out=tmp[k][:], in_=Tt[tb][:, cb, 2:514], func=AF.Identity,
                                            scale=convw[par][:, 4 + cb:5 + cb], bias=convb[par][:, cb:cb + 1]),
                     [b_T[tb], b_par[par]], [b_tmp[k]])
                P.op("dve",
                     lambda e: e.scalar_tensor_tensor(out=tmp[k][:], in0=Tt[tb][:, cb, 1:513],
                                                      scalar=convw[par][:, cb:cb + 1], in1=tmp[k][:],
                                                      op0=ALU.mult, op1=ALU.add),
                     [b_T[tb], b_par[par], b_tmp[k]], [b_tmp[k]])
                P.op("dve",
                     lambda e: e.scalar_tensor_tensor(out=tmp[k][:], in0=Tt[tb][:, cb, 3:515],
                                                      scalar=convw[par][:, 8 + cb:9 + cb], in1=tmp[k][:],
                                                      op0=ALU.mult, op1=ALU.add),
                     [b_T[tb], b_par[par], b_tmp[k]], [b_tmp[k]])
                return k

            def back(cb, k):
                P.op("pool",
                     lambda e: e.tensor_tensor(out=outA[:, cb, :], in0=tmp[k][:], in1=Gt[:, cb, :], op=ALU.mult),
                     [b_tmp[k], b_G[cb]], [b_outA])
            for cb in range(4):
                if ks is None:
                    k = front(cb)
                    if front_only:
                        tmp_hold.add(k)
                        held.append(k)
                        continue
                else:
                    k = ks[cb]
                    tmp_hold.discard(k)
                back(cb, k)
            return held

        def s1p2(it, deferred, reload=None):
            act = active_chunks(it)
            n0, n1 = act[0] * 128, (act[-1] + 1) * 128

            def bu(cb):
                ip = fm_group(CG_BU, cb, n0, n1)
                P.op("act", lambda e: e.activation(out=uz[:, cb, :], in_=ps[ip][:], func=AF.Gelu_apprx_tanh),
                     [b_ps[ip]], [b_uz[cb]])

            def ag(cb):
                iz = fm_group(CG_AZ, cb, n0, n1)
                ib = fm_group(CG_AB, cb, n0, n1)
                k = nxt("tmp", NTMP)
                P.op("act", lambda e: e.activation(out=tmp[k][:], in_=ps[iz][:], func=AF.Silu),
                     [b_ps[iz]], [b_tmp[k]])
                P.op("dve",
                     lambda e: e.tensor_tensor(out=Gt[:, cb, :], in0=ps[ib][:], in1=tmp[k][:], op=ALU.mult),
                     [b_ps[ib], b_tmp[k]], [b_G[cb]])

            def bz(cb):
                iz = fm_group(CG_BZ, cb, n0, n1)
                k = nxt("tmp", NTMP)
                P.op("act", lambda e: e.activation(out=tmp[k][:], in_=ps[iz][:], func=AF.Silu),
                     [b_ps[iz]], [b_tmp[k]])
                P.op("pool",
                     lambda e: e.tensor_tensor(out=uz[:, cb, :], in0=uz[:, cb, :], in1=tmp[k][:], op=ALU.mult),
                     [b_uz[cb], b_tmp[k]], [b_uz[cb]])
            for cb in range(4):
                bu(cb)
            if reload is not None:
                reload((CG_BU,))
            for cb in range(4):
                ag(cb)
                if cb == 0:
                    deferred[0][1]()
                    deferred[2][0]()
                if cb == 1:
                    deferred[1][1]()
                    deferred[3][0]()
                if cb == 3:
                    deferred[2][1]()
            if reload is not None:
                reload((CG_AZ, CG_AB))
            for cb in range(4):
                bz(cb)
                if cb == 0:
                    deferred[3][1]()
            if reload is not None:
                reload((CG_BZ,))

        def gating(it):
            l, g = items[it]
            par = l % 2
            ob = it % 2
            act = active_chunks(it)

            def one(h):
                ip = nxt("ps", NPS)
                P.op("pe",
                     [(lambda e, c=c: e.matmul(ps[ip][:, c * 128:(c + 1) * 128],
                                               vn[c][:, h * 128:(h + 1) * 128],
                                               sgwTb[par][:, h * 128:(h + 1) * 128],
                                               start=True, stop=True))
                      for c in act],
                     [b_vn[c] for c in act] + [b_par[par]], [b_ps[ip]])
                k = nxt("tmp", NTMP)
                P.op("dve",
                     lambda e: e.scalar_tensor_tensor(
                         out=sb_ap(tmp[k], 0, [[512, 128], [128, 4], [1, 128]]),
                         in0=sb_ap(ps[ip], 0, [[512, 128], [128, 4], [1, 128]]),
                         scalar=sgg[par][:, h:h + 1],
                         in1=sb_ap(Ct[par], h * 128, [[512, 128], [0, 4], [1, 128]]),
                         op0=ALU.mult, op1=ALU.add),
                     [b_ps[ip], b_par[par]], [b_tmp[k]])
                P.op("pool",
                     lambda e: e.tensor_tensor(out=outB[ob][:, h, :], in0=tmp[k][:],
                                               in1=uz[:, h, :], op=ALU.mult),
                     [b_tmp[k], b_uz[h]], [b_outB[ob]])
            for h in range(4):
                one(h)

        def load_xres_tile(it, c):
            l, g = items[it]
            src, bsrc = src_of(l)
            t = groups[g]["tile0"] + c
            i = nxt("xres", NXR)
            tsrc, tb = src_tile(l, t)
            assert l == 0 or tb is not None, "load emitted before its producer store"
            P.dma("sp", lambda e: e.dma_start(out=xres[i][:], in_=tsrc[t * 128:(t + 1) * 128, :]),
                  [tb] if tb is not None else [], [b_xres[i]])
            return i

        def s3(it, pre, hooks=(), inline_tails=False):
            l, g = items[it]
            ob = it % 2
            last_layer = (l == n_layers - 1)
            slots = list(pre)
            act = active_chunks(it)

            def half(c, hf, xi, zi):
                ip = nxt("ps", NPS)
                P.op("pe",
                     [(lambda e, ek=ek: e.matmul(
                         ps[ip][:],
                         (outA[:, ek, c * 128:(c + 1) * 128] if ek < 4
                          else outB[ob][:, ek - 4, c * 128:(c + 1) * 128]),
                         wout[:, ek, hf * 512:(hf + 1) * 512],
                         start=(ek == 0), stop=(ek == KC - 1)))
                      for ek in range(KC)],
                     [b_outA, b_outB[ob]] + b_wout, [b_ps[ip]])
                P.op("dve",
                     lambda e: e.scalar_tensor_tensor(
                         out=zt[zi][:, hf * 512:(hf + 1) * 512], in0=xres[xi][:, hf * 512:(hf + 1) * 512],
                         scalar=ALPHA, in1=ps[ip][:], op0=ALU.mult, op1=ALU.add),
                     [b_ps[ip], b_xres[xi]], [b_z[zi]])

            def skipped_tile(c):
                return

            def tile(c):
                t = groups[g]["tile0"] + c
                xi = slots[act.index(c)]
                zi = nxt("z", NZ)
                for hf in range(2):
                    half(c, hf, xi, zi)
                if len(slots) < len(act):
                    slots.append(load_xres_tile(it, act[len(slots)]))
                s = nxt("stz", NST)
                for hf in range(2):
                    P.op("dve",
                         lambda e, hf=hf: e.bn_stats(st_z[s][:, hf, :], zt[zi][:, hf * 512:(hf + 1) * 512]),
                         [b_z[zi]], [b_stz[s]])
                P.op("dve",
                     lambda e: e.bn_aggr(mv_z[s][:], sb_ap(st_z[s], 0, [[12, 128], [1, 12]])),
                     [b_stz[s]], [b_stz[s]])
                P.op("pool",
                     lambda e: e.tensor_scalar(out=rs_z[s][:], in0=mv_z[s][:, 1:2], scalar1=LN_EPS,
                                               scalar2=None, op0=ALU.add),
                     [b_stz[s]], [b_stz[s]])
                P.op("pool",
                     lambda e: e.tensor_tensor(out=rs_z[s][:], in0=rs_z[s][:], in1=neghalf[:, 0:1], op=ALU.pow),
                     [b_stz[s], b_const], [b_stz[s]])
                P.op("pool",
                     lambda e: e.tensor_tensor(out=nm_z[s][:], in0=mv_z[s][:, 0:1], in1=rs_z[s][:], op=ALU.mult),
                     [b_stz[s]], [b_stz[s]])
                P.op("pool",
                     lambda e: e.tensor_scalar(out=nm_z[s][:], in0=nm_z[s][:], scalar1=-1.0, scalar2=None,
                                               op0=ALU.mult),
                     [b_stz[s]], [b_stz[s]])
                def tail():
                    P.op("act",
                         lambda e: e.activation(out=zt[zi][:], in_=zt[zi][:], func=AF.Identity,
                                                scale=rs_z[s][:, 0:1], bias=nm_z[s][:, 0:1]),
                         [b_z[zi], b_stz[s]], [b_z[zi]])
                    for hf in range(2):
                        P.op("dve",
                             lambda e, hf=hf: e.tensor_tensor(out=zt[zi][:, hf * 512:(hf + 1) * 512],
                                                              in0=zt[zi][:, hf * 512:(hf + 1) * 512],
                                                              in1=lnG[:, hf * 512:(hf + 1) * 512], op=ALU.mult),
                             [b_z[zi], b_lnGB], [b_z[zi]])
                    P.op("pool", lambda e: e.tensor_tensor(out=zt[zi][:], in0=zt[zi][:], in1=lnB[:], op=ALU.add),
                         [b_z[zi], b_lnGB], [b_z[zi]])
                    if last_layer:
                        r0 = store_map[t]
                        if r0 is not None:
                            P.dma("sp", lambda e: e.dma_start(out=yout[r0:r0 + 128, :], in_=zt[zi][:]),
                                  [b_z[zi]], [b_yout[t]])
                    else:
                        dst = xs[l % 2]
                        P.dma("sp", lambda e: e.dma_start(out=dst[t * 128:(t + 1) * 128, :], in_=zt[zi][:]),
                              [b_z[zi]], [b_xs[l % 2][t]])
                        tile_src[t] = (dst, b_xs[l % 2][t])
                return tail
            out = []
            for c in range(4):
                if c < len(hooks):
                    hooks[c]()
                if c in act:
                    out.append(tile(c))
                    if inline_tails and len(out) >= 2:
                        out[-2]()
                else:
                    skipped_tile(c)
            if inline_tails:
                if out:
                    out[-1]()
                return []
            return out

        load_consts()
        xb_slots = load_xb(0)
        for cg in (CG_AIN, CG_AC):
            for kc in range(KC):
                load_win_slab(0, kc, cg)
        for tr in transposes(0, xb_slots):
            tr()
        for cg in (CG_BV, CG_BU, CG_AZ, CG_AB, CG_BZ):
            for kc in range(KC):
                load_win_slab(0, kc, cg)
        setup_layer_compute(0, *setup_layer_dma(0))
        for kc in range(KC):
            load_wout_slab(0, kc)
        load_ln(0)

        NI = len(items)
        s3_tails = []
        for it in range(NI + 1):
            cur = it if it < NI else None
            prev = it - 1 if it >= 1 else None
            nx = it + 1 if it + 1 < NI else None
            l_cur = items[cur][0] if cur is not None else None
            last_of_layer = cur is not None and items[cur][1] == NG - 1 and l_cur + 1 < n_layers
            first_of_layer = cur is not None and items[cur][1] == 0
            if nx is not None:
                xb_slots = load_xb(nx)
            pre = []
            if prev is not None:
                pre = [load_xres_tile(prev, c) for c in active_chunks(prev)[:NXR]]
            if cur is not None and items[cur][1] == NG - 3 and l_cur + 1 < n_layers:
                setup_slots = setup_layer_dma(l_cur + 1)
            if last_of_layer:
                setup_layer_compute(l_cur + 1, *setup_slots)
            vdef = []
            if cur is not None:
                s1p1(cur, s3_tails)
                if last_of_layer:
                    for cg in (CG_AIN, CG_AC):
                        for kc in range(KC):
                            load_win_slab(l_cur + 1, kc, cg)
            else:
                conv(prev, ks=last_conv_slots)
                for tl in s3_tails:
                    tl()
            s3_tails = []
            if cur is not None and items[cur][1] == 1 and l_cur >= 1:
                load_ln(l_cur)
            if cur is not None:
                vdef = s1v(cur)
                if last_of_layer:
                    for kc in range(KC):
                        load_win_slab(l_cur + 1, kc, CG_BV)
            if prev is not None and cur is not None:
                conv(prev)
            if cur is not None:
                for st_emit, _ in vdef[:2]:
                    st_emit()
                if last_of_layer:
                    def reload(cgs, l_next=l_cur + 1):
                        for cg in cgs:
                            for kc in range(KC):
                                load_win_slab(l_next, kc, cg)
                    s1p2(cur, vdef, reload)
                else:
                    s1p2(cur, vdef)
                gating(cur)
                if cur == NI - 1:
                    last_conv_slots = conv(cur, front_only=True)
            trs = transposes(nx, xb_slots) if nx is not None else []
            if prev is None:
                for tr in trs:
                    tr()
            if prev is not None:
                s3_tails = s3(prev, pre, trs, inline_tails=(cur is None))
                if first_of_layer and l_cur >= 1:
                    for kc in range(KC):
                        load_wout_slab(l_cur, kc)
        for tl in s3_tails:
            tl()

        P.wait_all("sp", b_yout)

        sems = {k: es.enter_context(nc.semaphore(k)) for k in P.sem_keys()}
        block = es.enter_context(nc.Block())

        @block.tensor
        def _(e):
            P.replay("pe", e, sems)

        @block.scalar
        def _(e):
            P.replay("act", e, sems)

        @block.vector
        def _(e):
            P.replay("dve", e, sems)

        @block.gpsimd
        def _(e):
            P.replay("pool", e, sems)

        @block.sync
        def _(e):
            P.replay("sp", e, sems)

    return nc, P


def layout_params(w_in, conv_w, conv_b, sg_w, sg_b, sg_norm_g, sg_norm_b, w_out, ln_g, ln_b):
    L = w_in.shape[0]
    f = np.float32
    convw = np.ascontiguousarray(
        conv_w.reshape(L, 3, 4, 128).transpose(0, 3, 1, 2).reshape(L, 128, 12)).astype(f)
    convb = np.ascontiguousarray(conv_b.reshape(L, 4, 128).transpose(0, 2, 1)).astype(f)
    sgwT = np.ascontiguousarray(sg_w.transpose(0, 3, 1, 2).reshape(L, 128, 512)).astype(f)
    sgb = np.ascontiguousarray(sg_b.reshape(L, 512)).astype(f)
    sgg = np.ascontiguousarray(sg_norm_g.reshape(L, 4, 128).transpose(0, 2, 1)).astype(f)
    sgnb = np.ascontiguousarray(sg_norm_b.reshape(L, 4, 128).transpose(0, 2, 1)).astype(f)
    return {
        "w_in": np.ascontiguousarray(w_in, dtype=f), "w_out": np.ascontiguousarray(w_out, dtype=f),
        "convw": convw, "convb": convb, "sgwT": sgwT, "sgb": sgb, "sgg": sgg, "sgnb": sgnb,
        "lng": np.ascontiguousarray(ln_g, dtype=f), "lnb": np.ascontiguousarray(ln_b, dtype=f),
        "ident": np.eye(128, dtype=f),
    }


N_CORES = 8
SEQ = 4096
DEC_SEQ = 16384
HALO = 256


def kernel(x_prompt, x_sample, w_in, conv_w, conv_b, sg_w, sg_b, sg_norm_g, sg_norm_b,
           w_out, ln_g, ln_b):
    x_prompt = np.asarray(x_prompt, dtype=np.float32)
    x_sample = np.asarray(x_sample, dtype=np.float32)
    params = layout_params(*(np.asarray(a, dtype=np.float32) for a in
                             (w_in, conv_w, conv_b, sg_w, sg_b, sg_norm_g, sg_norm_b, w_out, ln_g, ln_b)))
    n_s_chunks = SEQ // 128 + 4
    segs = [
        {"n_chunks": SEQ // 128},
        {"n_chunks": SEQ // 128},
        {"n_chunks": n_s_chunks, "mask_l_tok": HALO - 1, "mask_r_tok": HALO + SEQ, "halo": 2},
    ]
    n_tiles = sum(s["n_chunks"] for s in segs)
    store_map = []
    for t in range(n_tiles):
        if t < 64:
            store_map.append(t * 128)
        else:
            ts = t - 64
            store_map.append(8192 + (ts - 2) * 128 if 2 <= ts < 34 else None)
    nc, _ = build_program(DEPTH, segs, 3 * SEQ, store_map)

    in_maps = []
    for i in range(N_CORES):
        b, q = i // 4, i % 4
        xin = np.zeros((n_tiles * 128, D), dtype=np.float32)
        xin[0:SEQ] = x_prompt[2 * i]
        xin[SEQ:2 * SEQ] = x_prompt[2 * i + 1]
        lo = q * SEQ - HALO
        hi = (q + 1) * SEQ + HALO
        slo, shi = max(lo, 0), min(hi, DEC_SEQ)
        xin[2 * SEQ + (slo - lo):2 * SEQ + (shi - lo)] = x_sample[b, slo:shi]
        mask = np.zeros((128, 2), dtype=np.float32)
        mask[:, 0] = 0.0 if q == 0 else 1.0
        mask[:, 1] = 0.0 if q == 3 else 1.0
        m = {"xin": xin, "mask": mask}
        m.update(params)
        in_maps.append(m)

    res = run_bass_kernel_spmd(nc, in_maps, core_ids=list(range(N_CORES)))
    y_prompt = np.empty_like(x_prompt)
    y_sample = np.empty_like(x_sample)
    for i in range(N_CORES):
        b, q = i // 4, i % 4
        y = res.results[i]["yout"]
        y_prompt[2 * i] = y[0:SEQ]
        y_prompt[2 * i + 1] = y[SEQ:2 * SEQ]
        y_sample[b, q * SEQ:(q + 1) * SEQ] = y[2 * SEQ:3 * SEQ]
    return (y_prompt, y_sample)
```

```python
from contextlib import ExitStack

import numpy as np
import concourse.bass as bass
import concourse.mybir as mybir
from concourse.bass_utils import run_bass_kernel_spmd

F32 = mybir.dt.float32
BF16 = mybir.dt.bfloat16
AF = mybir.ActivationFunctionType
ALU = mybir.AluOpType

D = 1024
KC = 8
NCOLS = 3584
DEPTH = 4
ALPHA = (2.0 * DEPTH) ** 0.25
LN_EPS = 1e-5
CG_AIN, CG_AB, CG_AC, CG_AZ, CG_BU, CG_BV, CG_BZ = range(7)

ENGS = ("pe", "act", "dve", "pool", "sp")


class Buf:
    __slots__ = ("name", "w", "r")

    def __init__(self, name):
        self.name = name
        self.w = None
        self.r = {}


class Prog:
    def __init__(self, n_dma_sp=32, n_dma_pool=48):
        self.streams = {e: [] for e in ENGS}
        self.cnt = {e: 0 for e in ENGS}
        self.waited = {e: {} for e in ENGS}
        self.dma_pool = {
            "sp": ["dsp%d" % i for i in range(n_dma_sp)],
            "pool": ["dpl%d" % i for i in range(n_dma_pool)],
        }
        self.dma_next = {"sp": 0, "pool": 0}
        self.dma_cnt = {}
        self.n_ops = {e: 0 for e in ENGS}
        self.debug = False
        self.tags = {}

    def sem_keys(self):
        return list(ENGS) + self.dma_pool["sp"] + self.dma_pool["pool"]

    def _need(self, eng, deps):
        w = self.waited[eng]
        for s, v in deps:
            if v <= w.get(s, 0):
                continue
            w[s] = v
            self.streams[eng].append(("wait", s, v))

    def _deps(self, eng, reads, writes, is_dma):
        deps = []
        for b in reads:
            if b.w is not None and not (eng == "pe" and b.w[0] == "pe"):
                deps.append(b.w)
        for b in writes:
            if b.w is not None and not (eng == "pe" and b.w[0] == "pe"):
                deps.append(b.w)
            for s, v in b.r.items():
                if not (eng == "pe" and s == "pe"):
                    deps.append((s, v))
        return deps

    def op(self, eng, fns, reads=(), writes=()):
        if callable(fns):
            fns = [fns]
        self._need(eng, self._deps(eng, reads, writes, False))
        self.cnt[eng] += 1
        v = self.cnt[eng]
        if self.debug:
            import sys as _sys
            f = _sys._getframe(1)
            self.tags[(eng, v)] = "%s:%d" % (f.f_code.co_name, f.f_lineno)
        self.streams[eng].append(("op", fns, eng, 1))
        self.n_ops[eng] += len(fns)
        for b in reads:
            b.r[eng] = v
        for b in writes:
            b.w = (eng, v)
            b.r = {}
        return (eng, v)

    def dma(self, q, fn, reads=(), writes=()):
        deps = self._deps(q, reads, writes, True)
        pool = self.dma_pool[q]
        s = pool[self.dma_next[q] % len(pool)]
        self.dma_next[q] += 1
        k = self.dma_cnt.get(s, 0)
        if k > 0:
            deps.append((s, 16 * k))
        self._need(q, deps)
        k += 1
        self.dma_cnt[s] = k
        self.streams[q].append(("op", [fn], s, 16))
        self.n_ops[q] += 1
        for b in reads:
            b.r[s] = 16 * k
        for b in writes:
            b.w = (s, 16 * k)
            b.r = {}
        return (s, 16 * k)

    def wait_all(self, eng, bufs):
        deps = []
        for b in bufs:
            if b.w is not None:
                deps.append(b.w)
        self._need(eng, deps)

    def replay(self, eng, e, sems):
        for item in self.streams[eng]:
            if item[0] == "wait":
                e.wait_ge(sems[item[1]], item[2])
            else:
                _, fns, s, inc = item
                ins = None
                for fn in fns:
                    ins = fn(e)
                ins.then_inc(sems[s], inc)


def sb_ap(t, offset, dims):
    return bass.AP(t, offset, [list(d) for d in dims])


def build_program(n_layers, segs, n_out_tokens, store_map, debug=False):
    nc = bass.Bass("TRN2", target_bir_lowering=False)
    n_tiles = sum(s["n_chunks"] for s in segs)
    n_tok = n_tiles * 128
    assert all(s["n_chunks"] % 4 == 0 for s in segs)

    xin = nc.dram_tensor("xin", [n_tok, D], F32, kind="ExternalInput").ap()
    w_in_d = nc.dram_tensor("w_in", [n_layers, D, NCOLS], F32, kind="ExternalInput").ap()
    w_out_d = nc.dram_tensor("w_out", [n_layers, D, D], F32, kind="ExternalInput").ap()
    convw_d = nc.dram_tensor("convw", [n_layers, 128, 12], F32, kind="ExternalInput").ap()
    convb_d = nc.dram_tensor("convb", [n_layers, 128, 4], F32, kind="ExternalInput").ap()
    sgwT_d = nc.dram_tensor("sgwT", [n_layers, 128, 512], F32, kind="ExternalInput").ap()
    sgb_d = nc.dram_tensor("sgb", [n_layers, 512], F32, kind="ExternalInput").ap()
    sgg_d = nc.dram_tensor("sgg", [n_layers, 128, 4], F32, kind="ExternalInput").ap()
    sgnb_d = nc.dram_tensor("sgnb", [n_layers, 128, 4], F32, kind="ExternalInput").ap()
    lng_d = nc.dram_tensor("lng", [n_layers, D], F32, kind="ExternalInput").ap()
    lnb_d = nc.dram_tensor("lnb", [n_layers, D], F32, kind="ExternalInput").ap()
    ident_d = nc.dram_tensor("ident", [128, 128], F32, kind="ExternalInput").ap()
    mask_d = nc.dram_tensor("mask", [128, 2], F32, kind="ExternalInput").ap()
    yout = nc.dram_tensor("yout", [n_out_tokens, D], F32, kind="ExternalOutput").ap()
    xs = [nc.dram_tensor("xs%d" % i, [n_tok, D], F32).ap() for i in range(2)]

    P = Prog()
    P.debug = debug

    with ExitStack() as es:
        def sb(name, shape, dt):
            return es.enter_context(nc.sbuf_tensor(name, shape, dt))

        win = sb("win", [128, KC, NCOLS], BF16)
        wout = sb("wout", [128, KC, D], BF16)
        xT = sb("xT", [128, KC, 512], BF16)
        NXB = 4
        xb = [sb("xb%d" % i, [128, D], BF16) for i in range(NXB)]
        NXR = 3
        xres = [sb("xres%d" % i, [128, D], F32) for i in range(NXR)]
        NZ = 4
        zt = [sb("z%d" % i, [128, D], F32) for i in range(NZ)]
        Tt = [sb("T%d" % i, [128, 4, 516], F32) for i in range(2)]
        Gt = sb("G", [128, 4, 512], F32)
        outA = sb("outA", [128, 4, 512], BF16)
        outB = [sb("outB%d" % i, [128, 4, 512], BF16) for i in range(2)]
        uz = sb("uz", [128, 4, 512], F32)
        NTMP = 8
        tmp = [sb("tmp%d" % i, [128, 512], F32) for i in range(NTMP)]
        vn = [sb("vn%d" % i, [128, 512], BF16) for i in range(4)]
        lnG = sb("lnG", [128, D], F32)
        lnB = sb("lnB", [128, D], F32)
        identf = sb("identf", [128, 128], F32)
        identb = sb("identb", [128, 128], BF16)
        onesf = sb("onesf", [128, 128], F32)
        maskt = sb("maskt", [128, 2], F32)
        neghalf = sb("neghalf", [128, 4], F32)
        sgwTb = [sb("sgwTb%d" % i, [128, 512], BF16) for i in range(2)]
        Ct = [sb("C%d" % i, [128, 512], F32) for i in range(2)]
        convw = [sb("convw%d" % i, [128, 12], F32) for i in range(2)]
        convb = [sb("convb%d" % i, [128, 4], F32) for i in range(2)]
        sgg = [sb("sgg%d" % i, [128, 4], F32) for i in range(2)]
        sgnb = [sb("sgnb%d" % i, [128, 4], F32) for i in range(2)]
        NST = 4
        st_v = [sb("stv%d" % i, [128, 4, 6], F32) for i in range(NST)]
        mv_v = [sb("mvv%d" % i, [128, 4, 2], F32) for i in range(NST)]
        rs_v = [sb("rsv%d" % i, [128, 4], F32) for i in range(NST)]
        nm_v = [sb("nmv%d" % i, [128, 4], F32) for i in range(NST)]
        st_z = [sb("stz%d" % i, [128, 2, 6], F32) for i in range(NST)]
        mv_z = [sb("mvz%d" % i, [128, 2], F32) for i in range(NST)]
        rs_z = [sb("rsz%d" % i, [128, 1], F32) for i in range(NST)]
        nm_z = [sb("nmz%d" % i, [128, 1], F32) for i in range(NST)]

        NPS = 6
        ps = [es.enter_context(nc.psum_tensor("ps%d" % i, [128, 512], F32)) for i in range(NPS)]
        pt = [es.enter_context(nc.psum_tensor("pt%d" % i, [128, D], BF16)) for i in range(2)]

        b_win = [[Buf("win%d_%d" % (kc, cg)) for cg in range(7)] for kc in range(KC)]
        b_wout = [Buf("wout%d" % kc) for kc in range(KC)]
        b_xT = Buf("xT")
        b_xb = [Buf("xb%d" % i) for i in range(NXB)]
        b_xres = [Buf("xres%d" % i) for i in range(NXR)]
        b_z = [Buf("z%d" % i) for i in range(NZ)]
        b_T = [Buf("T%d" % i) for i in range(2)]
        b_G = [Buf("G%d" % i) for i in range(4)]
        b_outA = Buf("outA")
        b_outB = [Buf("outB%d" % i) for i in range(2)]
        b_uz = [Buf("uz%d" % i) for i in range(4)]
        b_tmp = [Buf("tmp%d" % i) for i in range(NTMP)]
        b_vn = [Buf("vn%d" % i) for i in range(4)]
        b_lnGB = Buf("lnGB")
        b_const = Buf("const")
        b_par = [Buf("par%d" % i) for i in range(2)]
        b_stv = [Buf("stv%d" % i) for i in range(NST)]
        b_stz = [Buf("stz%d" % i) for i in range(NST)]
        b_ps = [Buf("ps%d" % i) for i in range(NPS)]
        b_pt = [Buf("pt%d" % i) for i in range(2)]
        b_xs = [[Buf("xs%d_%d" % (i, t)) for t in range(n_tiles)] for i in range(2)]
        b_yout = [Buf("yout%d" % t) for t in range(n_tiles)]

        ctr = {"ps": 0, "pt": 0, "tmp": 0, "xb": 0, "xres": 0, "z": 0, "stv": 0, "stz": 0}

        tmp_hold = set()

        def nxt(name, n):
            while True:
                i = ctr[name] % n
                ctr[name] += 1
                if name == "tmp" and i in tmp_hold:
                    continue
                return i

        groups = []
        t0 = 0
        for si, s in enumerate(segs):
            ng = s["n_chunks"] // 4
            for g in range(ng):
                groups.append({
                    "tile0": t0 + 4 * g, "seg": si, "first": g == 0, "last": g == ng - 1,
                    "mask_l": (s.get("mask_l_tok") is not None and s["mask_l_tok"] // 512 == g),
                    "mask_l_col": (s["mask_l_tok"] % 512 + 2) if s.get("mask_l_tok") is not None else None,
                    "mask_r": (s.get("mask_r_tok") is not None and s["mask_r_tok"] // 512 == g),
                    "mask_r_col": (s["mask_r_tok"] % 512 + 2) if s.get("mask_r_tok") is not None else None,
                    "edge": [(min(4 * g + c, s["n_chunks"] - 1 - (4 * g + c))
                              if min(4 * g + c, s["n_chunks"] - 1 - (4 * g + c)) < s.get("halo", 0) else None)
                             for c in range(4)],
                })
            t0 += s["n_chunks"]
        NG = len(groups)

        def active_chunks(it):
            l, g = items[it]
            n_skip = 2 if l == n_layers - 1 else (0 if l == 0 else 1)
            return [c for c in range(4) if groups[g]["edge"][c] is None or groups[g]["edge"][c] >= n_skip]
        assert NG >= 4 or n_layers == 1
        items = [(l, g) for l in range(n_layers) for g in range(NG)]

        tile_src = {}

        def src_tile(l, t):
            if t in tile_src:
                return tile_src[t]
            return (xin, None)

        def src_of(l):
            return (xin, None) if l == 0 else (xs[(l - 1) % 2], b_xs[(l - 1) % 2])

        def load_consts():
            P.dma("sp", lambda e: e.dma_start(out=identf[:], in_=ident_d[:, :]), [], [b_const])
            P.dma("pool", lambda e: e.dma_start(out=identb[:], in_=ident_d[:, :]), [], [b_const])
            P.dma("sp", lambda e: e.dma_start(out=maskt[:], in_=mask_d[:, :]), [], [b_const])
            P.op("dve", lambda e: e.memset(onesf[:], 1.0), [], [b_const])
            P.op("dve", lambda e: e.memset(neghalf[:], -0.5), [], [b_const])

        def load_win_slab(l, kc, cg):
            P.dma("pool",
                  lambda e: e.dma_start(out=win[:, kc, cg * 512:(cg + 1) * 512],
                                        in_=w_in_d[l, kc * 128:(kc + 1) * 128, cg * 512:(cg + 1) * 512]),
                  [], [b_win[kc][cg]])

        def load_wout_slab(l, kc):
            P.dma("pool",
                  lambda e: e.dma_start(out=wout[:, kc, :], in_=w_out_d[l, kc * 128:(kc + 1) * 128, :]),
                  [], [b_wout[kc]])

        def load_ln(l):
            P.dma("sp", lambda e: e.dma_start(out=lnG[:], in_=lng_d[l, :].partition_broadcast(128)),
                  [], [b_lnGB])
            P.dma("sp", lambda e: e.dma_start(out=lnB[:], in_=lnb_d[l, :].partition_broadcast(128)),
                  [], [b_lnGB])

        def setup_layer_dma(l):
            par = l % 2
            bp = b_par[par]
            P.dma("sp", lambda e: e.dma_start(out=convw[par][:], in_=convw_d[l, :, :]), [], [bp])
            P.dma("sp", lambda e: e.dma_start(out=convb[par][:], in_=convb_d[l, :, :]), [], [bp])
            P.dma("sp", lambda e: e.dma_start(out=sgg[par][:], in_=sgg_d[l, :, :]), [], [bp])
            P.dma("sp", lambda e: e.dma_start(out=sgnb[par][:], in_=sgnb_d[l, :, :]), [], [bp])
            P.dma("pool", lambda e: e.dma_start(out=sgwTb[par][:], in_=sgwT_d[l, :, :]), [], [bp])
            i1 = nxt("tmp", NTMP)
            tmp_hold.add(i1)
            i2 = nxt("tmp", NTMP)
            tmp_hold.add(i2)
            P.dma("sp", lambda e: e.dma_start(out=tmp[i1][:], in_=sgwT_d[l, :, :]), [], [b_tmp[i1]])
            P.dma("sp", lambda e: e.dma_start(out=tmp[i2][:], in_=sgb_d[l, :].partition_broadcast(128)),
                  [], [b_tmp[i2]])
            return i1, i2

        def setup_layer_compute(l, i1, i2):
            par = l % 2
            bp = b_par[par]
            ip = nxt("ps", NPS)
            P.op("pe", lambda e: e.matmul(ps[ip][:], onesf[:], tmp[i1][:], start=True, stop=True),
                 [b_const, b_tmp[i1]], [b_ps[ip]])
            for h in range(4):
                P.op("dve",
                     lambda e, h=h: e.scalar_tensor_tensor(
                         out=Ct[par][:, h * 128:(h + 1) * 128], in0=ps[ip][:, h * 128:(h + 1) * 128],
                         scalar=sgnb[par][:, h:h + 1], in1=tmp[i2][:, h * 128:(h + 1) * 128],
                         op0=ALU.mult, op1=ALU.add),
                     [b_ps[ip], b_tmp[i2], bp], [bp])
            tmp_hold.discard(i1)
            tmp_hold.discard(i2)

        def load_xb(it):
            l, g = items[it]
            src, bsrc = src_of(l)
            slots = []

            def one(c):
                t = groups[g]["tile0"] + c
                i = nxt("xb", NXB)
                slots.append(i)
                tsrc, tb = src_tile(l, t)
                assert l == 0 or tb is not None, "load emitted before its producer store"
                P.dma("pool",
                      lambda e: e.dma_start(out=xb[i][:], in_=tsrc[t * 128:(t + 1) * 128, :]),
                      [tb] if tb is not None else [], [b_xb[i]])
            for c in range(4):
                one(c)
            return slots

        def transposes(it, slots):
            def one(c):
                i = slots[c]
                j = nxt("pt", 2)
                P.op("pe",
                     [(lambda e, kc=kc: e.transpose(pt[j][:, kc * 128:(kc + 1) * 128],
                                                    xb[i][:, kc * 128:(kc + 1) * 128], identb[:]))
                      for kc in range(KC)],
                     [b_xb[i], b_const], [b_pt[j]])
                P.op("dve",
                     lambda e: e.tensor_copy(
                         sb_ap(xT, c * 128, [[KC * 512, 128], [512, KC], [1, 128]]),
                         sb_ap(pt[j], 0, [[D, 128], [128, KC], [1, 128]])),
                     [b_pt[j]], [b_xT])
            return [lambda c=c: one(c) for c in range(4)]

        def fm_group(cg, cb, n0=0, n1=512):
            ip = nxt("ps", NPS)
            col = cg * 512 + cb * 128
            P.op("pe",
                 [(lambda e, kc=kc: e.matmul(ps[ip][:, n0:n1], win[:, kc, col:col + 128], xT[:, kc, n0:n1],
                                             start=(kc == 0), stop=(kc == KC - 1)))
                  for kc in range(KC)],
                 [b_win[kc][cg] for kc in range(KC)] + [b_xT], [b_ps[ip]])
            return ip

        def s1p1(it, tails):
            l, g = items[it]
            tb = it % 2

            def one(cb):
                ia = fm_group(CG_AIN, cb)
                ic = fm_group(CG_AC, cb)
                k = nxt("tmp", NTMP)
                P.op("act", lambda e: e.activation(out=tmp[k][:], in_=ps[ia][:], func=AF.Identity),
                     [b_ps[ia]], [b_tmp[k]])
                P.op("dve",
                     lambda e: e.tensor_tensor(out=Tt[tb][:, cb, 2:514], in0=ps[ic][:],
                                               in1=tmp[k][:], op=ALU.mult),
                     [b_ps[ic], b_tmp[k]], [b_T[tb]])
            for cb in range(4):
                one(cb)
                if cb < len(tails):
                    tails[cb]()
            gi = groups[g]
            if gi["mask_l"]:
                c0 = gi["mask_l_col"]
                P.op("dve",
                     lambda e: e.tensor_scalar(out=Tt[tb][:, :, c0:c0 + 1], in0=Tt[tb][:, :, c0:c0 + 1],
                                               scalar1=maskt[:, 0:1], scalar2=None, op0=ALU.mult),
                     [b_T[tb], b_const], [b_T[tb]])
            if gi["mask_r"]:
                c1 = gi["mask_r_col"]
                P.op("dve",
                     lambda e: e.tensor_scalar(out=Tt[tb][:, :, c1:c1 + 1], in0=Tt[tb][:, :, c1:c1 + 1],
                                               scalar1=maskt[:, 1:2], scalar2=None, op0=ALU.mult),
                     [b_T[tb], b_const], [b_T[tb]])
            pb = 1 - tb
            if gi["first"]:
                P.op("dve", lambda e: e.memset(Tt[tb][:, :, 1:2], 0.0), [], [b_T[tb]])
            else:
                P.op("dve", lambda e: e.tensor_copy(Tt[tb][:, :, 1:2], Tt[pb][:, :, 513:514]),
                     [b_T[pb]], [b_T[tb]])
                P.op("dve", lambda e: e.tensor_copy(Tt[pb][:, :, 514:515], Tt[tb][:, :, 2:3]),
                     [b_T[tb]], [b_T[pb]])
            if gi["last"]:
                P.op("dve", lambda e: e.memset(Tt[tb][:, :, 514:515], 0.0), [], [b_T[tb]])

        def s1v(it):
            act = active_chunks(it)

            def one(c):
                if c not in act:
                    return (lambda: None), (lambda: None)
                ip = nxt("ps", NPS)
                P.op("pe",
                     [(lambda e, kc=kc: e.matmul(ps[ip][:], xT[:, kc, c * 128:(c + 1) * 128],
                                                 win[:, kc, CG_BV * 512:(CG_BV + 1) * 512],
                                                 start=(kc == 0), stop=(kc == KC - 1)))
                      for kc in range(KC)],
                     [b_win[kc][CG_BV] for kc in range(KC)] + [b_xT], [b_ps[ip]])
                k = nxt("tmp", NTMP)
                tmp_hold.add(k)
                P.op("act", lambda e: e.activation(out=tmp[k][:], in_=ps[ip][:], func=AF.Gelu_apprx_tanh),
                     [b_ps[ip]], [b_tmp[k]])
                s = nxt("stv", NST)

                def stats():
                    for h in range(4):
                        P.op("dve", lambda e, h=h: e.bn_stats(st_v[s][:, h, :], tmp[k][:, h * 128:(h + 1) * 128]),
                             [b_tmp[k]], [b_stv[s]])
                    for h in range(4):
                        P.op("dve", lambda e, h=h: e.bn_aggr(mv_v[s][:, h, :], st_v[s][:, h, :]),
                             [b_stv[s]], [b_stv[s]])
                    P.op("pool",
                         lambda e: e.tensor_scalar(out=rs_v[s][:], in0=mv_v[s][:, :, 1], scalar1=LN_EPS,
                                                   scalar2=None, op0=ALU.add),
                         [b_stv[s]], [b_stv[s]])
                    P.op("pool",
                         lambda e: e.tensor_tensor(out=rs_v[s][:], in0=rs_v[s][:], in1=neghalf[:, 0:4], op=ALU.pow),
                         [b_stv[s], b_const], [b_stv[s]])
                    P.op("pool",
                         lambda e: e.tensor_tensor(out=nm_v[s][:], in0=mv_v[s][:, :, 0], in1=rs_v[s][:], op=ALU.mult),
                         [b_stv[s]], [b_stv[s]])
                    P.op("pool",
                         lambda e: e.tensor_scalar(out=nm_v[s][:], in0=nm_v[s][:], scalar1=-1.0, scalar2=None,
                                                   op0=ALU.mult),
                         [b_stv[s]], [b_stv[s]])

                def norm():
                    for h in range(4):
                        P.op("act",
                             lambda e, h=h: e.activation(
                                 out=vn[c][:, h * 128:(h + 1) * 128], in_=tmp[k][:, h * 128:(h + 1) * 128],
                                 func=AF.Identity, scale=rs_v[s][:, h:h + 1], bias=nm_v[s][:, h:h + 1]),
                             [b_tmp[k], b_stv[s]], [b_vn[c]])
                    tmp_hold.discard(k)
                return stats, norm
            return [one(c) for c in range(4)]

        def conv(it):
            l, g = items[it]
            tb = it % 2
            par = l % 2

            def one(cb):
                k = nxt("tmp", NTMP)
                P.op("act",
                     lambda e: e.activation(out=tmp[k][:], in_=Tt[tb][:, cb, 2:514], func=AF.Identity,
                                            scale=convw[par][:, 4 + cb:5 + cb], bias=convb[par][:, cb:cb + 1]),
                     [b_T[tb], b_par[par]], [b_tmp[k]])
                P.op("dve",
                     lambda e: e.scalar_tensor_tensor(out=tmp[k][:], in0=Tt[tb][:, cb, 1:513],
                                                      scalar=convw[par][:, cb:cb + 1], in1=tmp[k][:],
                                                      op0=ALU.mult, op1=ALU.add),
                     [b_T[tb], b_par[par], b_tmp[k]], [b_tmp[k]])
                P.op("dve",
                     lambda e: e.scalar_tensor_tensor(out=tmp[k][:], in0=Tt[tb][:, cb, 3:515],
                                                      scalar=convw[par][:, 8 + cb:9 + cb], in1=tmp[k][:],
                                                      op0=ALU.mult, op1=ALU.add),
                     [b_T[tb], b_par[par], b_tmp[k]], [b_tmp[k]])
                P.op("pool",
                     lambda e: e.tensor_tensor(out=outA[:, cb, :], in0=tmp[k][:], in1=Gt[:, cb, :], op=ALU.mult),
                     [b_tmp[k], b_G[cb]], [b_outA])
            for cb in range(4):
                one(cb)

        def s1p2(it, deferred, reload=None):
            act = active_chunks(it)
            n0, n1 = act[0] * 128, (act[-1] + 1) * 128

            def bu(cb):
                ip = fm_group(CG_BU, cb, n0, n1)
                P.op("act", lambda e: e.activation(out=uz[:, cb, :], in_=ps[ip][:], func=AF.Gelu_apprx_tanh),
                     [b_ps[ip]], [b_uz[cb]])

            def ag(cb):
                iz = fm_group(CG_AZ, cb, n0, n1)
                ib = fm_group(CG_AB, cb, n0, n1)
                k = nxt("tmp", NTMP)
                P.op("act", lambda e: e.activation(out=tmp[k][:], in_=ps[iz][:], func=AF.Silu),
                     [b_ps[iz]], [b_tmp[k]])
                P.op("dve",
                     lambda e: e.tensor_tensor(out=Gt[:, cb, :], in0=ps[ib][:], in1=tmp[k][:], op=ALU.mult),
                     [b_ps[ib], b_tmp[k]], [b_G[cb]])

            def bz(cb):
                iz = fm_group(CG_BZ, cb, n0, n1)
                k = nxt("tmp", NTMP)
                P.op("act", lambda e: e.activation(out=tmp[k][:], in_=ps[iz][:], func=AF.Silu),
                     [b_ps[iz]], [b_tmp[k]])
                P.op("pool",
                     lambda e: e.tensor_tensor(out=uz[:, cb, :], in0=uz[:, cb, :], in1=tmp[k][:], op=ALU.mult),
                     [b_uz[cb], b_tmp[k]], [b_uz[cb]])
            for cb in range(4):
                bu(cb)
            if reload is not None:
                reload((CG_BU,))
            for cb in range(4):
                ag(cb)
                if cb == 0:
                    deferred[0][1]()
                    deferred[2][0]()
                if cb == 1:
                    deferred[1][1]()
                    deferred[3][0]()
                if cb == 3:
                    deferred[2][1]()
            if reload is not None:
                reload((CG_AZ, CG_AB))
            for cb in range(4):
                bz(cb)
                if cb == 0:
                    deferred[3][1]()
            if reload is not None:
                reload((CG_BZ,))

        def gating(it):
            l, g = items[it]
            par = l % 2
            ob = it % 2
            act = active_chunks(it)

            def one(h):
                ip = nxt("ps", NPS)
                P.op("pe",
                     [(lambda e, c=c: e.matmul(ps[ip][:, c * 128:(c + 1) * 128],
                                               vn[c][:, h * 128:(h + 1) * 128],
                                               sgwTb[par][:, h * 128:(h + 1) * 128],
                                               start=True, stop=True))
                      for c in act],
                     [b_vn[c] for c in act] + [b_par[par]], [b_ps[ip]])
                k = nxt("tmp", NTMP)
                P.op("dve",
                     lambda e: e.scalar_tensor_tensor(
                         out=sb_ap(tmp[k], 0, [[512, 128], [128, 4], [1, 128]]),
                         in0=sb_ap(ps[ip], 0, [[512, 128], [128, 4], [1, 128]]),
                         scalar=sgg[par][:, h:h + 1],
                         in1=sb_ap(Ct[par], h * 128, [[512, 128], [0, 4], [1, 128]]),
                         op0=ALU.mult, op1=ALU.add),
                     [b_ps[ip], b_par[par]], [b_tmp[k]])
                P.op("pool",
                     lambda e: e.tensor_tensor(out=outB[ob][:, h, :], in0=tmp[k][:],
                                               in1=uz[:, h, :], op=ALU.mult),
                     [b_tmp[k], b_uz[h]], [b_outB[ob]])
            for h in range(4):
                one(h)

        def load_xres_tile(it, c):
            l, g = items[it]
            src, bsrc = src_of(l)
            t = groups[g]["tile0"] + c
            i = nxt("xres", NXR)
            tsrc, tb = src_tile(l, t)
            assert l == 0 or tb is not None, "load emitted before its producer store"
            P.dma("sp", lambda e: e.dma_start(out=xres[i][:], in_=tsrc[t * 128:(t + 1) * 128, :]),
                  [tb] if tb is not None else [], [b_xres[i]])
            return i

        def s3(it, pre, hooks=(), inline_tails=False):
            l, g = items[it]
            ob = it % 2
            last_layer = (l == n_layers - 1)
            slots = list(pre)
            act = active_chunks(it)

            def half(c, hf, xi, zi):
                ip = nxt("ps", NPS)
                P.op("pe",
                     [(lambda e, ek=ek: e.matmul(
                         ps[ip][:],
                         (outA[:, ek, c * 128:(c + 1) * 128] if ek < 4
                          else outB[ob][:, ek - 4, c * 128:(c + 1) * 128]),
                         wout[:, ek, hf * 512:(hf + 1) * 512],
                         start=(ek == 0), stop=(ek == KC - 1)))
                      for ek in range(KC)],
                     [b_outA, b_outB[ob]] + b_wout, [b_ps[ip]])
                P.op("dve",
                     lambda e: e.scalar_tensor_tensor(
                         out=zt[zi][:, hf * 512:(hf + 1) * 512], in0=xres[xi][:, hf * 512:(hf + 1) * 512],
                         scalar=ALPHA, in1=ps[ip][:], op0=ALU.mult, op1=ALU.add),
                     [b_ps[ip], b_xres[xi]], [b_z[zi]])

            def skipped_tile(c):
                return

            def tile(c):
                t = groups[g]["tile0"] + c
                xi = slots[act.index(c)]
                zi = nxt("z", NZ)
                for hf in range(2):
                    half(c, hf, xi, zi)
                if len(slots) < len(act):
                    slots.append(load_xres_tile(it, act[len(slots)]))
                s = nxt("stz", NST)
                for hf in range(2):
                    P.op("dve",
                         lambda e, hf=hf: e.bn_stats(st_z[s][:, hf, :], zt[zi][:, hf * 512:(hf + 1) * 512]),
                         [b_z[zi]], [b_stz[s]])
                P.op("dve",
                     lambda e: e.bn_aggr(mv_z[s][:], sb_ap(st_z[s], 0, [[12, 128], [1, 12]])),
                     [b_stz[s]], [b_stz[s]])
                P.op("pool",
                     lambda e: e.tensor_scalar(out=rs_z[s][:], in0=mv_z[s][:, 1:2], scalar1=LN_EPS,
                                               scalar2=None, op0=ALU.add),
                     [b_stz[s]], [b_stz[s]])
                P.op("pool",
                     lambda e: e.tensor_tensor(out=rs_z[s][:], in0=rs_z[s][:], in1=neghalf[:, 0:1], op=ALU.pow),
                     [b_stz[s], b_const], [b_stz[s]])
                P.op("pool",
                     lambda e: e.tensor_tensor(out=nm_z[s][:], in0=mv_z[s][:, 0:1], in1=rs_z[s][:], op=ALU.mult),
                     [b_stz[s]], [b_stz[s]])
                P.op("pool",
                     lambda e: e.tensor_scalar(out=nm_z[s][:], in0=nm_z[s][:], scalar1=-1.0, scalar2=None,
                                               op0=ALU.mult),
                     [b_stz[s]], [b_stz[s]])
                def tail():
                    P.op("act",
                         lambda e: e.activation(out=zt[zi][:], in_=zt[zi][:], func=AF.Identity,
                                                scale=rs_z[s][:, 0:1], bias=nm_z[s][:, 0:1]),
                         [b_z[zi], b_stz[s]], [b_z[zi]])
                    for hf in range(2):
                        P.op("dve",
                             lambda e, hf=hf: e.tensor_tensor(out=zt[zi][:, hf * 512:(hf + 1) * 512],
                                                              in0=zt[zi][:, hf * 512:(hf + 1) * 512],
                                                              in1=lnG[:, hf * 512:(hf + 1) * 512], op=ALU.mult),
                             [b_z[zi], b_lnGB], [b_z[zi]])
                    P.op("pool", lambda e: e.tensor_tensor(out=zt[zi][:], in0=zt[zi][:], in1=lnB[:], op=ALU.add),
                         [b_z[zi], b_lnGB], [b_z[zi]])
                    if last_layer:
                        r0 = store_map[t]
                        if r0 is not None:
                            P.dma("sp", lambda e: e.dma_start(out=yout[r0:r0 + 128, :], in_=zt[zi][:]),
                                  [b_z[zi]], [b_yout[t]])
                    else:
                        dst = xs[l % 2]
                        P.dma("sp", lambda e: e.dma_start(out=dst[t * 128:(t + 1) * 128, :], in_=zt[zi][:]),
                              [b_z[zi]], [b_xs[l % 2][t]])
                        tile_src[t] = (dst, b_xs[l % 2][t])
                return tail
            out = []
            for c in range(4):
                if c < len(hooks):
                    hooks[c]()
                if c in act:
                    out.append(tile(c))
                    if inline_tails and len(out) >= 2:
                        out[-2]()
                else:
                    skipped_tile(c)
            if inline_tails:
                if out:
                    out[-1]()
                return []
            return out

        load_consts()
        xb_slots = load_xb(0)
        for cg in (CG_AIN, CG_AC):
            for kc in range(KC):
                load_win_slab(0, kc, cg)
        for tr in transposes(0, xb_slots):
            tr()
        for cg in (CG_BV, CG_BU, CG_AZ, CG_AB, CG_BZ):
            for kc in range(KC):
                load_win_slab(0, kc, cg)
        setup_layer_compute(0, *setup_layer_dma(0))
        for kc in range(KC):
            load_wout_slab(0, kc)
        load_ln(0)

        NI = len(items)
        s3_tails = []
        for it in range(NI + 1):
            cur = it if it < NI else None
            prev = it - 1 if it >= 1 else None
            nx = it + 1 if it + 1 < NI else None
            l_cur = items[cur][0] if cur is not None else None
            last_of_layer = cur is not None and items[cur][1] == NG - 1 and l_cur + 1 < n_layers
            first_of_layer = cur is not None and items[cur][1] == 0
            if nx is not None:
                xb_slots = load_xb(nx)
            pre = []
            if prev is not None:
                pre = [load_xres_tile(prev, c) for c in active_chunks(prev)[:NXR]]
            if cur is not None and items[cur][1] == NG - 3 and l_cur + 1 < n_layers:
                setup_slots = setup_layer_dma(l_cur + 1)
            if last_of_layer:
                setup_layer_compute(l_cur + 1, *setup_slots)
            vdef = []
            if cur is not None:
                s1p1(cur, s3_tails)
                if last_of_layer:
                    for cg in (CG_AIN, CG_AC):
                        for kc in range(KC):
                            load_win_slab(l_cur + 1, kc, cg)
            else:
                conv(prev)
                for tl in s3_tails:
                    tl()
            s3_tails = []
            if cur is not None and items[cur][1] == 1 and l_cur >= 1:
                load_ln(l_cur)
            if cur is not None:
                vdef = s1v(cur)
                if last_of_layer:
                    for kc in range(KC):
                        load_win_slab(l_cur + 1, kc, CG_BV)
            if prev is not None and cur is not None:
                conv(prev)
            if cur is not None:
                for st_emit, _ in vdef[:2]:
                    st_emit()
                if last_of_layer:
                    def reload(cgs, l_next=l_cur + 1):
                        for cg in cgs:
                            for kc in range(KC):
                                load_win_slab(l_next, kc, cg)
                    s1p2(cur, vdef, reload)
                else:
                    s1p2(cur, vdef)
                gating(cur)
            trs = transposes(nx, xb_slots) if nx is not None else []
            if prev is None:
                for tr in trs:
                    tr()
            if prev is not None:
                s3_tails = s3(prev, pre, trs, inline_tails=(cur is None))
                if first_of_layer and l_cur >= 1:
                    for kc in range(KC):
                        load_wout_slab(l_cur, kc)
        for tl in s3_tails:
            tl()

        P.wait_all("sp", b_yout)

        sems = {k: es.enter_context(nc.semaphore(k)) for k in P.sem_keys()}
        block = es.enter_context(nc.Block())

        @block.tensor
        def _(e):
            P.replay("pe", e, sems)

        @block.scalar
        def _(e):
            P.replay("act", e, sems)

        @block.vector
        def _(e):
            P.replay("dve", e, sems)

        @block.gpsimd
        def _(e):
            P.replay("pool", e, sems)

        @block.sync
        def _(e):
            P.replay("sp", e, sems)

    return nc, P


def layout_params(w_in, conv_w, conv_b, sg_w, sg_b, sg_norm_g, sg_norm_b, w_out, ln_g, ln_b):
    L = w_in.shape[0]
    f = np.float32
    convw = np.ascontiguousarray(
        conv_w.reshape(L, 3, 4, 128).transpose(0, 3, 1, 2).reshape(L, 128, 12)).astype(f)
    convb = np.ascontiguousarray(conv_b.reshape(L, 4, 128).transpose(0, 2, 1)).astype(f)
    sgwT = np.ascontiguousarray(sg_w.transpose(0, 3, 1, 2).reshape(L, 128, 512)).astype(f)
    sgb = np.ascontiguousarray(sg_b.reshape(L, 512)).astype(f)
    sgg = np.ascontiguousarray(sg_norm_g.reshape(L, 4, 128).transpose(0, 2, 1)).astype(f)
    sgnb = np.ascontiguousarray(sg_norm_b.reshape(L, 4, 128).transpose(0, 2, 1)).astype(f)
    return {
        "w_in": np.ascontiguousarray(w_in, dtype=f), "w_out": np.ascontiguousarray(w_out, dtype=f),
        "convw": convw, "convb": convb, "sgwT": sgwT, "sgb": sgb, "sgg": sgg, "sgnb": sgnb,
        "lng": np.ascontiguousarray(ln_g, dtype=f), "lnb": np.ascontiguousarray(ln_b, dtype=f),
        "ident": np.eye(128, dtype=f),
    }


N_CORES = 8
SEQ = 4096
DEC_SEQ = 16384
HALO = 256


def kernel(x_prompt, x_sample, w_in, conv_w, conv_b, sg_w, sg_b, sg_norm_g, sg_norm_b,
           w_out, ln_g, ln_b):
    x_prompt = np.asarray(x_prompt, dtype=np.float32)
    x_sample = np.asarray(x_sample, dtype=np.float32)
    params = layout_params(*(np.asarray(a, dtype=np.float32) for a in
                             (w_in, conv_w, conv_b, sg_w, sg_b, sg_norm_g, sg_norm_b, w_out, ln_g, ln_b)))
    n_s_chunks = SEQ // 128 + 4
    segs = [
        {"n_chunks": SEQ // 128},
        {"n_chunks": SEQ // 128},
        {"n_chunks": n_s_chunks, "mask_l_tok": HALO - 1, "mask_r_tok": HALO + SEQ, "halo": 2},
    ]
    n_tiles = sum(s["n_chunks"] for s in segs)
    store_map = []
    for t in range(n_tiles):
        if t < 64:
            store_map.append(t * 128)
        else:
            ts = t - 64
            store_map.append(8192 + (ts - 2) * 128 if 2 <= ts < 34 else None)
    nc, _ = build_program(DEPTH, segs, 3 * SEQ, store_map)

    in_maps = []
    for i in range(N_CORES):
        b, q = i // 4, i % 4
        xin = np.zeros((n_tiles * 128, D), dtype=np.float32)
        xin[0:SEQ] = x_prompt[2 * i]
        xin[SEQ:2 * SEQ] = x_prompt[2 * i + 1]
        lo = q * SEQ - HALO
        hi = (q + 1) * SEQ + HALO
        slo, shi = max(lo, 0), min(hi, DEC_SEQ)
        xin[2 * SEQ + (slo - lo):2 * SEQ + (shi - lo)] = x_sample[b, slo:shi]
        mask = np.zeros((128, 2), dtype=np.float32)
        mask[:, 0] = 0.0 if q == 0 else 1.0
        mask[:, 1] = 0.0 if q == 3 else 1.0
        m = {"xin": xin, "mask": mask}
        m.update(params)
        in_maps.append(m)

    res = run_bass_kernel_spmd(nc, in_maps, core_ids=list(range(N_CORES)))
    y_prompt = np.empty_like(x_prompt)
    y_sample = np.empty_like(x_sample)
    for i in range(N_CORES):
        b, q = i // 4, i % 4
        y = res.results[i]["yout"]
        y_prompt[2 * i] = y[0:SEQ]
        y_prompt[2 * i + 1] = y[SEQ:2 * SEQ]
        y_sample[b, q * SEQ:(q + 1) * SEQ] = y[2 * SEQ:3 * SEQ]
    return (y_prompt, y_sample)
```
